# Optimizing a Trainium2 kernel written in Bass

```python
import jax, jax.numpy as jnp
from jax import lax
import numpy as np

D_MODEL = 2048
BATCH = 4
SEQ = 4096
DEPTH = 2

D_MIX = D_MODEL
WIDTH_A = D_MIX // 2
WIDTH_B = D_MIX - WIDTH_A
CHUNK = 128
HEADS_A = 8
DH_A = WIDTH_A // HEADS_A
HEADS_B = 16
DH_B = WIDTH_B // HEADS_B
Q_BLOCK = 128
EPS = 1e-6
SPLIT_SIZES = (WIDTH_A, WIDTH_A, WIDTH_A, WIDTH_B, WIDTH_B, WIDTH_B, WIDTH_B, HEADS_B)
D_IN = sum(SPLIT_SIZES)

kernel_name = "hybrid_gmlp_forgetting_attn_block"


def _split_points(sizes):
    pts, acc = [], 0
    for s in sizes[:-1]:
        acc += s
        pts.append(acc)
    return pts


def rms_norm(x, g):
    xf = x.astype(jnp.float32)
    y = xf * lax.rsqrt(jnp.mean(xf * xf, axis=-1, keepdims=True) + EPS)
    return (y * g.astype(jnp.float32)).astype(x.dtype)


def layer_norm(x, g, b):
    xf = x.astype(jnp.float32)
    mu = jnp.mean(xf, axis=-1, keepdims=True)
    xc = xf - mu
    var = jnp.mean(xc * xc, axis=-1, keepdims=True)
    y = xc * lax.rsqrt(var + EPS) * g.astype(jnp.float32) + b.astype(jnp.float32)
    return y.astype(x.dtype)


def chunked_spatial_gating(u, v, ln_g, ln_b, w_s, b_s):
    bsz, s, _ = v.shape
    n = s // CHUNK
    v = layer_norm(v, ln_g, ln_b)
    vc = v.reshape(bsz, n, CHUNK, HEADS_A, DH_A)
    causal = jnp.tril(jnp.ones((CHUNK, CHUNK), dtype=bool))
    w = jnp.where(causal[None], w_s, jnp.zeros_like(w_s))
    mixed = jnp.einsum('hts,bnshd->bnthd', w, vc) + b_s.T[None, None, :, :, None]
    return u * mixed.reshape(bsz, s, WIDTH_A)


def forgetting_attention(q, k, v, f_logit, b_f):
    bsz, s, _ = q.shape
    nb = s // Q_BLOCK
    to_heads = lambda t: t.reshape(bsz, s, HEADS_B, DH_B).transpose(0, 2, 1, 3)
    q, k, v = to_heads(q), to_heads(k), to_heads(v)
    log_f = jax.nn.log_sigmoid(f_logit.astype(jnp.float32) + b_f.astype(jnp.float32))
    F = jnp.cumsum(log_f, axis=1).transpose(0, 2, 1)
    qb = q.reshape(bsz, HEADS_B, nb, Q_BLOCK, DH_B).transpose(2, 0, 1, 3, 4)
    Fb = F.reshape(bsz, HEADS_B, nb, Q_BLOCK).transpose(2, 0, 1, 3)
    k_pos = jnp.arange(s)
    scale = DH_B ** -0.5

    def block(args):
        qi, Fi, i = args
        q_pos = i * Q_BLOCK + jnp.arange(Q_BLOCK)
        logits = jnp.einsum('bhqd,bhkd->bhqk', qi, k).astype(jnp.float32) * scale
        logits = logits + (Fi[..., :, None] - F[..., None, :])
        logits = jnp.where(q_pos[:, None] >= k_pos[None, :], logits, -jnp.inf)
        p = jax.nn.softmax(logits, axis=-1)
        return jnp.einsum('bhqk,bhkd->bhqd', p.astype(v.dtype), v)

    out = lax.map(block, (qb, Fb, jnp.arange(nb)))
    return out.transpose(1, 0, 3, 2, 4).reshape(bsz, s, WIDTH_B)


def setup_inputs(seed: int = 0) -> dict:
    key = jax.random.key(seed)
    ks = jax.random.split(key, 16)
    f32 = jnp.float32
    nrm = lambda k, shape: jax.random.normal(k, shape, dtype=f32)
    row_scale = (1.0 / jnp.sqrt(jnp.arange(1, CHUNK + 1, dtype=f32)))[None, None, :, None]
    b_f = jnp.linspace(1.0, 5.0, HEADS_B, dtype=f32)[None, :] + 0.1 * nrm(ks[11], (DEPTH, HEADS_B))
    return {
        "x": nrm(ks[0], (BATCH, SEQ, D_MODEL)),
        "c": nrm(ks[1], (BATCH, D_MODEL)),
        "norm_g": 1.0 + 0.1 * nrm(ks[2], (DEPTH, D_MODEL)),
        "w_ada": nrm(ks[3], (DEPTH, D_MODEL, 3 * D_MODEL)) * D_MODEL ** -0.5,
        "b_ada": 0.01 * nrm(ks[4], (DEPTH, 3 * D_MODEL)),
        "w_in": nrm(ks[5], (DEPTH, D_MODEL, D_IN)) * D_MODEL ** -0.5,
        "ln_v_g": 1.0 + 0.1 * nrm(ks[6], (DEPTH, WIDTH_A)),
        "ln_v_b": 0.01 * nrm(ks[7], (DEPTH, WIDTH_A)),
        "w_s": nrm(ks[8], (DEPTH, HEADS_A, CHUNK, CHUNK)) * row_scale,
        "b_s": 1.0 + 0.1 * nrm(ks[9], (DEPTH, HEADS_A, CHUNK)),
        "b_f": b_f,
        "w_out": nrm(ks[10], (DEPTH, D_MIX, D_MODEL)) * D_MIX ** -0.5,
        "final_g": 1.0 + 0.1 * nrm(ks[12], (D_MODEL,)),
    }


def reference(x, c, norm_g, w_ada, b_ada, w_in, ln_v_g, ln_v_b, w_s, b_s, b_f, w_out, final_g):
    cond = jax.nn.silu(c)
    pts = _split_points(SPLIT_SIZES)
    for l in range(DEPTH):
        mod = cond @ w_ada[l] + b_ada[l]
        shift, scale, gate = jnp.split(mod, 3, axis=-1)
        h = rms_norm(x, norm_g[l]) * (1.0 + scale[:, None, :]) + shift[:, None, :]
        z = h @ w_in[l]
        u_a, v_a, g_a, q_b, k_b, v_b, g_b, f_b = jnp.split(z, pts, axis=-1)
        y_a = chunked_spatial_gating(jax.nn.gelu(u_a, approximate=False),
                                     jax.nn.gelu(v_a, approximate=False),
                                     ln_v_g[l], ln_v_b[l], w_s[l], b_s[l]) * jax.nn.silu(g_a)
        y_b = forgetting_attention(q_b, k_b, v_b, f_b, b_f[l]) * jax.nn.silu(g_b)
        y = jnp.concatenate([y_a, y_b], axis=-1) @ w_out[l]
        x = x + gate[:, None, :] * y
    return rms_norm(x, final_g)
```

```python
import contextlib
import numpy as np
import ml_dtypes
import concourse.bass as bass
import concourse.mybir as mybir
from concourse.bass_utils import run_bass_kernel_spmd

F32 = mybir.dt.float32
BF16 = mybir.dt.bfloat16
AF = mybir.ActivationFunctionType
ALU = mybir.AluOpType
NPBF = ml_dtypes.bfloat16

D = 2048
SEQ = 4096
BATCH = 4
DEPTH = 2
WA = 1024
WB = 1024
HB = 16
DH = 64
DIN = 7184
EPS = 1e-6
NCORES = 8
TOK = 2048
MASKV = -30000.0


class Reg:
    __slots__ = ("name", "writer", "readers")

    def __init__(self, name):
        self.name = name
        self.writer = None
        self.readers = []


class Sched:
    def __init__(self, nc, es, ndma=8):
        self.nc = nc
        self.eng = {"pe": nc.tensor, "act": nc.scalar, "dve": nc.vector, "pool": nc.gpsimd, "sp": nc.sync}
        self.sem = {e: es.enter_context(nc.semaphore("sem_" + e)) for e in ("pe", "act", "dve", "pool")}
        self.cnt = {e: 0 for e in self.sem}
        self.known = {e: {} for e in self.eng}
        self.dsem = {}
        for q in ("sp", "pool", "act"):
            self.dsem[q] = [[es.enter_context(nc.semaphore("d_%s%d" % (q, i))), 0] for i in range(ndma)]
        self.dptr = {q: 0 for q in self.dsem}
        self.out_tokens = []

    def _deps(self, e, reads, writes):
        need = {}
        for r in reads:
            if r.writer is not None:
                self._add(need, e, r.writer, raw=True)
        for w in writes:
            for t in w.readers:
                self._add(need, e, t, raw=False)
            if w.writer is not None:
                self._add(need, e, w.writer, raw=False)
        for (sem, val) in need.values():
            k = id(sem)
            if self.known[e].get(k, 0) < val:
                self.eng[e].wait_ge(sem, val)
                self.known[e][k] = val

    def _add(self, need, e, tok, raw):
        sem, val, src = tok
        if src == e:
            if e == "pe":
                return
            if not raw:
                return
        k = id(sem)
        if k not in need or need[k][1] < val:
            need[k] = (sem, val)

    def _record(self, tok, reads, writes):
        for r in reads:
            r.readers.append(tok)
        for w in writes:
            w.writer = tok
            w.readers = []

    def op(self, e, fn, reads=(), writes=(), sig=True):
        self._deps(e, reads, writes)
        inst = fn(self.eng[e])
        if sig:
            self.cnt[e] += 1
            inst.then_inc(self.sem[e], 1)
            tok = (self.sem[e], self.cnt[e], e)
        else:
            tok = (self.sem[e], self.cnt[e] + 1, e)
        self._record(tok, reads, writes)
        return tok

    def dma(self, q, out, in_, reads=(), writes=(), is_output=False):
        slot = self.dsem[q][self.dptr[q] % len(self.dsem[q])]
        self.dptr[q] += 1
        sem, prev = slot
        k = id(sem)
        if prev > 0 and self.known[q].get(k, 0) < prev:
            self.eng[q].wait_ge(sem, prev)
            self.known[q][k] = prev
        self._deps(q, reads, writes)
        self.eng[q].dma_start(out=out, in_=in_).then_inc(sem, 16)
        slot[1] = prev + 16
        tok = (sem, prev + 16, "dma_" + q)
        self._record(tok, reads, writes)
        if is_output:
            self.out_tokens.append(tok)
        return tok

    def finish(self):
        need = {}
        for (sem, val, _) in self.out_tokens:
            k = id(sem)
            if k not in need or need[k][1] < val:
                need[k] = (sem, val)
        for (sem, val) in need.values():
            self.eng["sp"].wait_ge(sem, val)

    def mm(self, out, lhsT, rhs, start, stop, reads, writes, sig):
        return self.op("pe", lambda e: e.matmul(out, lhsT=lhsT, rhs=rhs, start=start, stop=stop),
                       reads=reads, writes=writes, sig=sig)


def _dram(nc, name, shape, dtype, kind):
    return nc.dram_tensor(name, list(shape), dtype, kind=kind).ap()


L1_ORDER = [0, 1, 4, 5, 2, 3, 6, 7, 8, 9, 10, 11, 12, 13, 14]
L1_KIND = {0: "u", 1: "u", 2: "v", 3: "v", 4: "ga", 5: "ga", 6: "q", 7: "q", 8: "k", 9: "k",
           10: "V", 11: "V", 12: "gb", 13: "gb", 14: "f"}
HALF = 1024
NT = HALF // 128


def build_l1():
    nc = bass.Bass("TRN2", target_bir_lowering=False)
    x = _dram(nc, "x", [TOK, D], F32, "ExternalInput")
    cT = _dram(nc, "cT", [128, 16], F32, "ExternalInput")
    ng = _dram(nc, "ng", [128, D], F32, "ExternalInput")
    wada = _dram(nc, "wada", [D, 3 * D], F32, "ExternalInput")
    bada = _dram(nc, "bada", [1, 3 * D], F32, "ExternalInput")
    win = _dram(nc, "win", [D, DIN], F32, "ExternalInput")
    lng = _dram(nc, "lng", [128, WA], F32, "ExternalInput")
    lnb = _dram(nc, "lnb", [128, WA], F32, "ExternalInput")
    wsT = _dram(nc, "wsT", [128, 8, 128], F32, "ExternalInput")
    tri = _dram(nc, "tri", [128, 128], F32, "ExternalInput")
    bs = _dram(nc, "bs", [128, 8], F32, "ExternalInput")
    bfc = _dram(nc, "bfc", [16, 1], F32, "ExternalInput")
    identb = _dram(nc, "identb", [128, 128], BF16, "ExternalInput")
    ya = _dram(nc, "ya", [TOK, WA], BF16, "ExternalOutput")
    qT = _dram(nc, "qT", [WB, TOK], BF16, "ExternalOutput")
    kT = _dram(nc, "kT", [WB, TOK], BF16, "ExternalOutput")
    vv = _dram(nc, "vv", [TOK, HB * 65], BF16, "ExternalOutput")
    sgb = _dram(nc, "sgb", [TOK, WB], BF16, "ExternalOutput")
    lfT = _dram(nc, "lfT", [16, TOK], F32, "ExternalOutput")
    gate = _dram(nc, "gate", [1, D], F32, "ExternalOutput")

    es = contextlib.ExitStack()
    with es:
        S = Sched(nc, es)
        sb = lambda name, shape, dtype: es.enter_context(nc.sbuf_tensor(name, list(shape), dtype))
        pst = lambda name, shape, dtype: es.enter_context(nc.psum_tensor(name, list(shape), dtype))
        hT = sb("hT", [128, 16, HALF], BF16)
        ug = sb("ug", [128, NT, WA], BF16)
        gv = sb("gv", [128, NT, WA], BF16)
        wbuf = [sb("wbuf%d" % i, [128, 16, 512], BF16) for i in range(2)]
        Amul = sb("Amul", [128, D], F32)
        shiftB = sb("shiftB", [128, D], F32)
        xin = [sb("xin%d" % i, [128, D], F32) for i in range(2)]
        hb = [sb("hb%d" % i, [128, D], BF16) for i in range(2)]
        lng_t = sb("lng_t", [128, WA], F32)
        lnb_t = sb("lnb_t", [128, WA], F32)
        lntmp = sb("lntmp", [128, WA], F32)
        vn = [sb("vn%d" % i, [128, WA], BF16) for i in range(2)]
        yat = [sb("yat%d" % i, [128, WA], BF16) for i in range(2)]
        stg = [sb("stg%d" % i, [128, 512], BF16) for i in range(4)]
        vst = [sb("vst%d" % i, [128, 8, 65], BF16) for i in range(2)]
        sgt = [sb("sgt%d" % i, [128, 512], BF16) for i in range(2)]
        fst = [sb("fst%d" % i, [16, 512], F32) for i in range(2)]
        fe = sb("fe", [16, 512], F32)
        cT_t = sb("cT_t", [128, 16], F32)
        condb = sb("condb", [128, 16], BF16)
        wsT_f = sb("wsT_f", [128, 8, 128], F32)
        wsT_b = sb("wsT_b", [128, 8, 128], BF16)
        tri_t = sb("tri_t", [128, 128], F32)
        bs_t = sb("bs_t", [128, 8], F32)
        nbf = sb("nbf", [16, 1], F32)
        ident = sb("ident", [128, 128], BF16)
        ones_f = sb("ones_f", [1, 128], F32)
        bada_t = sb("bada_t", [1, 512], F32)
        small = sb("small", [128, 8], F32)
        stats = sb("stats", [128, 2, 6], F32)
        mv = sb("mv", [128, 2], F32)
        junk = sb("junk", [128, D], BF16)
        acc = [pst("acc%d" % i, [128, 512], F32) for i in range(4)]
        tp = [pst("tp%d" % i, [128, 1024], BF16) for i in range(2)]
        gps = pst("gps", [128, 1024], F32)

        R = Reg
        hTR = [R("hT%d" % t) for t in range(NT)]
        ugR = [R("ug%d" % t) for t in range(NT)]
        gvR = [R("gv%d" % t) for t in range(NT)]
        wR = [R("w0"), R("w1")]
        AR, shR = R("A"), R("sh")
        xinR = [R("xin0"), R("xin1")]
        hbR = [R("hb0"), R("hb1")]
        lnR, lntmpR = R("ln"), R("lntmp")
        vnR = [R("vn0"), R("vn1")]
        yatR = [R("yat0"), R("yat1")]
        stgR = [R("stg%d" % i) for i in range(4)]
        vstR = [R("vst0"), R("vst1")]
        sgtR = [R("sgt0"), R("sgt1")]
        fstR = [R("fst0"), R("fst1")]
        feR = R("fe")
        cR, condR, wsfR, wsbR, triR, bsR, nbfR, idR, onesR, badaR = (R(n) for n in
            ("c", "cond", "wsf", "wsb", "tri", "bs", "nbf", "id", "ones", "bada"))
        smallR, statsR, mvR, junkR = R("small"), R("stats"), R("mv"), R("junk")
        accR = [R("acc%d" % i) for i in range(4)]
        tpR = [R("tp0"), R("tp1")]
        gpsR = R("gps")

        S.dma("sp", cT_t[:], cT[:, :], writes=[cR])
        S.dma("sp", Amul[:], ng[:, :], writes=[AR])
        S.dma("sp", lng_t[:], lng[:, :], writes=[lnR])
        S.dma("sp", lnb_t[:], lnb[:, :], writes=[lnR])
        S.dma("sp", wsT_f[:], wsT[:, :, :], writes=[wsfR])
        S.dma("sp", tri_t[:], tri[:, :], writes=[triR])
        S.dma("sp", bs_t[:], bs[:, :], writes=[bsR])
        S.dma("sp", nbf[:], bfc[:, :], writes=[nbfR])
        S.dma("sp", ident[:], identb[:, :], writes=[idR])
        S.op("dve", lambda e: e.memset(ones_f[:], 1.0), writes=[onesR])
        for i in range(2):
            S.op("pool", lambda e, i=i: e.memset(vst[i][:], 1.0), writes=[vstR[i]])
        S.op("act", lambda e: e.activation(out=condb[:], in_=cT_t[:], func=AF.Silu), reads=[cR], writes=[condR])
        S.op("dve", lambda e: e.tensor_scalar(out=nbf[:], in0=nbf[:], scalar1=-1.0, scalar2=None, op0=ALU.mult),
             reads=[nbfR], writes=[nbfR])
        for h in range(8):
            S.op("dve", lambda e, h=h: e.tensor_tensor(out=wsT_b[:, h, :], in0=wsT_f[:, h, :], in1=tri_t[:], op=ALU.mult),
                 reads=[wsfR, triR], writes=[wsbR])

        wslot = [0]
        acci = [0]

        def next_acc():
            a = acci[0] % 4
            acci[0] += 1
            return a

        wlist = []
        for piece in range(3):
            for j in range(4):
                col0 = piece * D + j * 512
                wlist.append((wada[:, col0:col0 + 512], 512))
        for half in range(TOK // HALF):
            for c in L1_ORDER:
                ncols = min(512, DIN - c * 512)
                wlist.append((win[:, c * 512:c * 512 + ncols], ncols))
        wissued = [0]

        def w_issue_upto(i):
            while wissued[0] <= i and wissued[0] < len(wlist):
                n = wissued[0]
                src, ncols = wlist[n]
                S.dma("pool", wbuf[n % 2][:, :, 0:ncols], src.rearrange("(k p) c -> p k c", p=128), writes=[wR[n % 2]])
                wissued[0] += 1

        def next_w():
            i = wslot[0]
            wslot[0] += 1
            w_issue_upto(i + 1)
            return i % 2

        modrow = xin[0]
        for piece in range(3):
            for j in range(4):
                col0 = piece * D + j * 512
                s = next_w()
                S.dma("sp", bada_t[:], bada[0:1, col0:col0 + 512], writes=[badaR])
                a = next_acc()
                for k in range(16):
                    S.mm(acc[a][0:1, :], lhsT=condb[:, k:k + 1], rhs=wbuf[s][:, k, :], start=(k == 0), stop=(k == 15),
                         reads=[condR, wR[s]], writes=[accR[a]], sig=(k == 15))
                S.op("dve", lambda e, a=a, j=j: e.tensor_tensor(out=modrow[0:1, j * 512:(j + 1) * 512], in0=acc[a][0:1, :],
                                                                in1=bada_t[:], op=ALU.add),
                     reads=[accR[a], badaR], writes=[xinR[0]])
            if piece == 2:
                S.dma("sp", gate[0:1, :], modrow[0:1, :], reads=[xinR[0]], is_output=True)
            else:
                for j in range(4):
                    a = next_acc()
                    S.mm(acc[a][:, :], lhsT=ones_f[0:1, :], rhs=modrow[0:1, j * 512:(j + 1) * 512], start=True, stop=True,
                         reads=[onesR, xinR[0]], writes=[accR[a]], sig=True)
                    if piece == 0:
                        S.op("act", lambda e, a=a, j=j: e.copy(out=shiftB[:, j * 512:(j + 1) * 512], in_=acc[a][:, :]),
                             reads=[accR[a]], writes=[shR])
                    else:
                        S.op("dve", lambda e, a=a, j=j: e.scalar_tensor_tensor(
                            out=Amul[:, j * 512:(j + 1) * 512], in0=acc[a][:, :], scalar=1.0,
                            in1=Amul[:, j * 512:(j + 1) * 512], op0=ALU.add, op1=ALU.mult),
                             reads=[accR[a], AR], writes=[AR])

        stgi = [0]

        def next_stg():
            s = stgi[0] % 4
            stgi[0] += 1
            return s

        cnt2 = {"v": 0, "sg": 0, "f": 0, "x": 0, "tp": 0, "vn": 0, "yat": 0}

        def rot(key, n):
            v = cnt2[key] % n
            cnt2[key] += 1
            return v

        for half in range(TOK // HALF):
            tb = half * HALF
            for t in range(NT):
                xi = rot("x", 2)
                r0 = tb + t * 128
                S.dma("sp", xin[xi][:], x[r0:r0 + 128, :], writes=[xinR[xi]])
                S.op("act", lambda e, xi=xi: e.activation(out=junk[:], in_=xin[xi][:], func=AF.Square,
                                                           accum_out=small[:, 0:1]),
                     reads=[xinR[xi]], writes=[junkR, smallR])
                S.op("act", lambda e: e.activation(out=small[:, 1:2], in_=small[:, 0:1], func=AF.Sqrt,
                                                   scale=1.0 / D, bias=EPS),
                     reads=[smallR], writes=[smallR])
                S.op("dve", lambda e: e.reciprocal(out=small[:, 2:3], in_=small[:, 1:2]), reads=[smallR], writes=[smallR])
                S.op("dve", lambda e, xi=xi: e.scalar_tensor_tensor(out=xin[xi][:], in0=xin[xi][:], scalar=small[:, 2:3],
                                                                    in1=Amul[:], op0=ALU.mult, op1=ALU.mult),
                     reads=[xinR[xi], smallR, AR], writes=[xinR[xi]])
                S.op("pool", lambda e, xi=xi: e.tensor_tensor(out=hb[xi][:], in0=xin[xi][:], in1=shiftB[:], op=ALU.add),
                     reads=[xinR[xi], shR], writes=[hbR[xi]])
                for g in range(4):
                    ti = rot("tp", 2)
                    for kk in range(4):
                        k = g * 4 + kk
                        S.op("pe", lambda e, ti=ti, kk=kk, k=k, xi=xi: e.transpose(
                            tp[ti][:, kk * 128:(kk + 1) * 128], hb[xi][:, k * 128:(k + 1) * 128], ident[:]),
                             reads=[hbR[xi], idR], writes=[tpR[ti]], sig=(kk == 3))
                    eng = "act" if g % 2 == 0 else "dve"
                    src = tp[ti][:, 0:512].rearrange("p (k t) -> p k t", t=128)
                    dst = hT[:, g * 4:(g + 1) * 4, t * 128:(t + 1) * 128]
                    if eng == "act":
                        S.op("act", lambda e, src=src, dst=dst: e.copy(out=dst, in_=src), reads=[tpR[ti]], writes=[hTR[t]])
                    else:
                        S.op("dve", lambda e, src=src, dst=dst: e.tensor_copy(out=dst, in_=src), reads=[tpR[ti]],
                             writes=[hTR[t]])

            pending_ga = []

            def emit_groupA_mm(t):
                vi = t % 2
                for h in range(8):
                    S.mm(gps[:, h * 128:(h + 1) * 128], lhsT=wsT_b[:, h, :], rhs=vn[vi][:, h * 128:(h + 1) * 128],
                         start=True, stop=True, reads=[wsbR, vnR[vi]], writes=[gpsR], sig=(h == 7))
                yi = rot("yat", 2)
                for h in range(8):
                    S.op("dve", lambda e, h=h, yi=yi, t=t: e.scalar_tensor_tensor(
                        out=yat[yi][:, h * 128:(h + 1) * 128], in0=gps[:, h * 128:(h + 1) * 128], scalar=bs_t[:, h:h + 1],
                        in1=ug[:, t, h * 128:(h + 1) * 128], op0=ALU.add, op1=ALU.mult),
                         reads=[gpsR, bsR, ugR[t]], writes=[yatR[yi]])
                r0 = tb + t * 128
                S.dma("sp", ya[r0:r0 + 128, :], yat[yi][:], reads=[yatR[yi]], is_output=True)

            def emit_ln(t):
                vi = t % 2
                for i in range(2):
                    S.op("dve", lambda e, i=i, t=t: e.bn_stats(out=stats[:, i, :], in_=gv[:, t, i * 512:(i + 1) * 512]),
                         reads=[gvR[t]], writes=[statsR])
                S.op("dve", lambda e: e.bn_aggr(out=mv[:], in_=stats[:].rearrange("p a b -> p (a b)")),
                     reads=[statsR], writes=[mvR])
                S.op("act", lambda e: e.activation(out=small[:, 4:5], in_=mv[:, 1:2], func=AF.Sqrt, scale=1.0, bias=EPS),
                     reads=[mvR], writes=[smallR])
                S.op("dve", lambda e: e.reciprocal(out=small[:, 5:6], in_=small[:, 4:5]), reads=[smallR], writes=[smallR])
                S.op("dve", lambda e, t=t: e.tensor_scalar(out=lntmp[:], in0=gv[:, t, :], scalar1=mv[:, 0:1],
                                                            scalar2=small[:, 5:6], op0=ALU.subtract, op1=ALU.mult),
                     reads=[gvR[t], mvR, smallR], writes=[lntmpR])
                S.op("pool", lambda e: e.tensor_tensor(out=lntmp[:], in0=lntmp[:], in1=lng_t[:], op=ALU.mult),
                     reads=[lntmpR, lnR], writes=[lntmpR])
                S.op("pool", lambda e, vi=vi: e.tensor_tensor(out=vn[vi][:], in0=lntmp[:], in1=lnb_t[:], op=ALU.add),
                     reads=[lntmpR, lnR], writes=[vnR[vi]])

            for c in L1_ORDER:
                kind = L1_KIND[c]
                ncols = min(512, DIN - c * 512)
                s = next_w()
                cc = c % 2
                if kind in ("u", "v", "ga", "V", "gb"):
                    for t in range(NT):
                        a = next_acc()
                        for k in range(16):
                            S.mm(acc[a][:, :], lhsT=hT[:, k, t * 128:(t + 1) * 128], rhs=wbuf[s][:, k, :],
                                 start=(k == 0), stop=(k == 15), reads=[hTR[t], wR[s]], writes=[accR[a]], sig=(k == 15))
                        r0 = tb + t * 128
                        if kind == "u":
                            S.op("act", lambda e, a=a, t=t, cc=cc: e.activation(out=ug[:, t, cc * 512:(cc + 1) * 512],
                                                                                 in_=acc[a][:, :], func=AF.Gelu),
                                 reads=[accR[a]], writes=[ugR[t]])
                        elif kind == "v":
                            S.op("act", lambda e, a=a, t=t, cc=cc: e.activation(out=gv[:, t, cc * 512:(cc + 1) * 512],
                                                                                 in_=acc[a][:, :], func=AF.Gelu),
                                 reads=[accR[a]], writes=[gvR[t]])
                            if cc == 1:
                                emit_ln(t)
                                pending_ga.append(t)
                                if len(pending_ga) > 1:
                                    emit_groupA_mm(pending_ga.pop(0))
                        elif kind == "ga":
                            si = rot("sg", 2)
                            S.op("act", lambda e, a=a, si=si: e.activation(out=sgt[si][:], in_=acc[a][:, :], func=AF.Silu),
                                 reads=[accR[a]], writes=[sgtR[si]])
                            S.op("pool", lambda e, si=si, t=t, cc=cc: e.tensor_tensor(
                                out=ug[:, t, cc * 512:(cc + 1) * 512], in0=ug[:, t, cc * 512:(cc + 1) * 512],
                                in1=sgt[si][:], op=ALU.mult),
                                 reads=[sgtR[si], ugR[t]], writes=[ugR[t]])
                        elif kind == "V":
                            vi = rot("v", 2)
                            S.op("dve", lambda e, a=a, vi=vi: e.tensor_copy(
                                out=vst[vi][:, :, 0:64], in_=acc[a][:, :].rearrange("p (h d) -> p h d", d=64)),
                                 reads=[accR[a]], writes=[vstR[vi]])
                            S.dma("sp", vv[r0:r0 + 128, cc * 520:(cc + 1) * 520],
                                  vst[vi][:].rearrange("p h d -> p (h d)"), reads=[vstR[vi]], is_output=True)
                        elif kind == "gb":
                            si = next_stg()
                            S.op("act", lambda e, a=a, si=si: e.activation(out=stg[si][:], in_=acc[a][:, :], func=AF.Silu),
                                 reads=[accR[a]], writes=[stgR[si]])
                            S.dma("sp", sgb[r0:r0 + 128, cc * 512:(cc + 1) * 512], stg[si][:], reads=[stgR[si]],
                                  is_output=True)
                    if kind == "v" and cc == 1:
                        while pending_ga:
                            emit_groupA_mm(pending_ga.pop(0))
                elif kind in ("q", "k"):
                    dst = qT if kind == "q" else kT
                    for m in range(4):
                        for T in range(HALF // 512):
                            a = next_acc()
                            for k in range(16):
                                S.mm(acc[a][:, :], lhsT=wbuf[s][:, k, m * 128:(m + 1) * 128],
                                     rhs=hT[:, k, T * 512:(T + 1) * 512], start=(k == 0), stop=(k == 15),
                                     reads=[wR[s]] + hTR[T * 4:(T + 1) * 4], writes=[accR[a]], sig=(k == 15))
                            si = next_stg()
                            if kind == "q":
                                S.op("dve", lambda e, a=a, si=si: e.tensor_scalar(out=stg[si][:], in0=acc[a][:, :],
                                                                                  scalar1=0.125, scalar2=None, op0=ALU.mult),
                                     reads=[accR[a]], writes=[stgR[si]])
                            else:
                                S.op("dve", lambda e, a=a, si=si: e.tensor_copy(out=stg[si][:], in_=acc[a][:, :]),
                                     reads=[accR[a]], writes=[stgR[si]])
                            row0 = cc * 512 + m * 128
                            col0 = tb + T * 512
                            S.dma("sp", dst[row0:row0 + 128, col0:col0 + 512], stg[si][:], reads=[stgR[si]],
                                  is_output=True)
                else:
                    for T in range(HALF // 512):
                        a = next_acc()
                        for k in range(16):
                            S.mm(acc[a][0:16, :], lhsT=wbuf[s][:, k, 0:16], rhs=hT[:, k, T * 512:(T + 1) * 512],
                                 start=(k == 0), stop=(k == 15), reads=[wR[s]] + hTR[T * 4:(T + 1) * 4],
                                 writes=[accR[a]], sig=(k == 15))
                        fi = rot("f", 2)
                        S.op("act", lambda e, a=a: e.activation(out=fe[:], in_=acc[a][0:16, :], func=AF.Exp,
                                                                scale=-1.0, bias=nbf[:, 0:1]),
                             reads=[accR[a], nbfR], writes=[feR])
                        S.op("act", lambda e: e.activation(out=fe[:], in_=fe[:], func=AF.Ln, scale=1.0, bias=1.0),
                             reads=[feR], writes=[feR])
                        S.op("dve", lambda e, fi=fi: e.tensor_scalar(out=fst[fi][:], in0=fe[:], scalar1=-1.0, scalar2=None,
                                                                     op0=ALU.mult),
                             reads=[feR], writes=[fstR[fi]])
                        col0 = tb + T * 512
                        S.dma("sp", lfT[:, col0:col0 + 512], fst[fi][:], reads=[fstR[fi]], is_output=True)
        S.finish()
    return nc


NH2 = 8
NBLK = SEQ // 128
QC = 512
NQC = SEQ // QC


def build_l2():
    nc = bass.Bass("TRN2", target_bir_lowering=False)
    qT = _dram(nc, "qT", [NH2 * 64, SEQ], BF16, "ExternalInput")
    kT = _dram(nc, "kT", [NH2 * 64, SEQ], BF16, "ExternalInput")
    vv = _dram(nc, "vv", [SEQ, NH2 * 65], BF16, "ExternalInput")
    sgb = _dram(nc, "sgb", [SEQ, NH2 * 64], BF16, "ExternalInput")
    lfT = _dram(nc, "lfT", [NH2, SEQ], F32, "ExternalInput")
    identf = _dram(nc, "identf", [128, 128], F32, "ExternalInput")
    identb = _dram(nc, "identb", [128, 128], BF16, "ExternalInput")
    maskd = _dram(nc, "maskd", [128, 128], BF16, "ExternalInput")
    yb = _dram(nc, "yb", [SEQ, NH2 * 64], BF16, "ExternalOutput")

    es = contextlib.ExitStack()
    with es:
        S = Sched(nc, es)
        sb = lambda name, shape, dtype: es.enter_context(nc.sbuf_tensor(name, list(shape), dtype))
        pst = lambda name, shape, dtype: es.enter_context(nc.psum_tensor(name, list(shape), dtype))
        qa = [sb("qa%d" % i, [67, SEQ], BF16) for i in range(2)]
        ka = [sb("ka%d" % i, [67, SEQ], BF16) for i in range(2)]
        va = sb("va", [128, NBLK, NH2 * 65], BF16)
        sg = sb("sg", [128, NBLK, NH2 * 64], BF16)
        ybh = [sb("ybh%d" % i, [128, NBLK, 64], BF16) for i in range(2)]
        lf = sb("lf", [NH2, SEQ], F32)
        Ft = sb("Ft", [NH2, SEQ], F32)
        Fr = lf
        Fhi = sb("Fhi", [NH2, SEQ], BF16)
        Fmid = sb("Fmid", [NH2, SEQ], BF16)
        Flo = sb("Flo", [NH2, SEQ], BF16)
        negF = sb("negF", [128, NBLK, NH2], F32)
        idf = sb("idf", [128, 128], F32)
        idb = sb("idb", [128, 128], BF16)
        mk = sb("mk", [128, 128], BF16)
        pt = [sb("pt%d" % i, [128, QC], BF16) for i in range(4)]
        rc = sb("rc", [128, 8], F32)
        st = [pst("st%d" % i, [128, 512], F32) for i in range(4)]
        ac = [pst("ac%d" % i, [128, 512], F32) for i in range(2)]
        tps = pst("tps", [128, 512], F32)

        R = Reg
        qaR = [R("qa0"), R("qa1")]
        kaR = [R("ka0"), R("ka1")]
        vaR, sgR = R("va"), R("sg")
        ybhR = [R("ybh0"), R("ybh1")]
        lfR, FtR, FhR, FmR, FlR, negFR = (R(n) for n in ("lf", "Ft", "Fh", "Fm", "Fl", "negF"))
        FrR = lfR
        idfR, idbR, mkR = R("idf"), R("idb"), R("mk")
        ptR = [R("pt%d" % i) for i in range(4)]
        rcR = R("rc")
        stR = [R("st%d" % i) for i in range(4)]
        acR = [R("ac0"), R("ac1")]
        tpsR = R("tps")

        S.dma("sp", lf[:], lfT[:, :], writes=[lfR])
        S.dma("sp", idf[:], identf[:, :], writes=[idfR])
        S.dma("sp", idb[:], identb[:, :], writes=[idbR])
        S.dma("sp", mk[:], maskd[:, :], writes=[mkR])
        for i in range(2):
            S.op("pool", lambda e, i=i: e.memset(ka[i][64:67, :], 1.0), writes=[kaR[i]])
        for g in range(4):
            b0 = g * 8
            S.dma("sp", va[:, b0:b0 + 8, :], vv[b0 * 128:(b0 + 8) * 128, :].rearrange("(t p) c -> p t c", p=128),
                  writes=[vaR])
        for g in range(4):
            b0 = g * 8
            S.dma("pool", sg[:, b0:b0 + 8, :], sgb[b0 * 128:(b0 + 8) * 128, :].rearrange("(t p) c -> p t c", p=128),
                  writes=[sgR])
        S.op("dve", lambda e: e.tensor_tensor_scan(out=Ft[:], data0=lf[:], data1=lf[:], initial=0.0,
                                                   op0=ALU.add, op1=ALU.min),
             reads=[lfR], writes=[FtR])
        S.op("dve", lambda e: e.tensor_copy(out=Fhi[:], in_=Ft[:]), reads=[FtR], writes=[FhR])
        S.op("dve", lambda e: e.tensor_tensor(out=Fr[:], in0=Ft[:], in1=Fhi[:], op=ALU.subtract),
             reads=[FtR, FhR], writes=[FrR])
        S.op("dve", lambda e: e.tensor_copy(out=Fmid[:], in_=Fr[:]), reads=[FrR], writes=[FmR])
        S.op("dve", lambda e: e.tensor_tensor(out=Fr[:], in0=Fr[:], in1=Fmid[:], op=ALU.subtract),
             reads=[FrR, FmR], writes=[FrR])
        S.op("dve", lambda e: e.tensor_copy(out=Flo[:], in_=Fr[:]), reads=[FrR], writes=[FlR])
        for g in range(NBLK // 4):
            for jj in range(4):
                j = g * 4 + jj
                S.op("pe", lambda e, j=j, jj=jj: e.transpose(tps[:, jj * NH2:(jj + 1) * NH2], Ft[:, j * 128:(j + 1) * 128],
                                                             idf[0:NH2, 0:NH2]),
                     reads=[FtR, idfR], writes=[tpsR], sig=(jj == 3))
            S.op("dve", lambda e, g=g: e.tensor_scalar(
                out=negF[:, g * 4:(g + 1) * 4, :], in0=tps[:, 0:4 * NH2].rearrange("p (j h) -> p j h", h=NH2),
                scalar1=-1.0, scalar2=None, op0=ALU.mult),
                 reads=[tpsR], writes=[negFR])

        its = []
        for h in range(NH2):
            for c in range(NQC):
                for j in range(4 * c + 4):
                    its.append((h, c, j))
        state = {"n": 0}

        def load_head(h):
            s = h % 2
            S.dma("sp", qa[s][0:64, :], qT[h * 64:(h + 1) * 64, :], writes=[qaR[s]])
            S.dma("sp", ka[s][0:64, :], kT[h * 64:(h + 1) * 64, :], writes=[kaR[s]])
            S.dma("sp", qa[s][64:65, :], Fhi[h:h + 1, :], reads=[FhR], writes=[qaR[s]])
            S.dma("sp", qa[s][65:66, :], Fmid[h:h + 1, :], reads=[FmR], writes=[qaR[s]])
            S.dma("sp", qa[s][66:67, :], Flo[h:h + 1, :], reads=[FlR], writes=[qaR[s]])

        def emit_scores(n):
            h, c, j = its[n]
            s = h % 2
            b = n % 4
            qb0 = max(j, 4 * c)
            q0, q1 = qb0 * 128, (4 * c + 4) * 128
            N = q1 - q0
            diag = (j >= 4 * c)
            S.mm(st[b][:, 0:N], lhsT=ka[s][:, j * 128:(j + 1) * 128], rhs=qa[s][:, q0:q1], start=True, stop=(not diag),
                 reads=[kaR[s], qaR[s]], writes=[stR[b]], sig=(not diag))
            if diag:
                S.mm(st[b][:, 0:128], lhsT=idb[:], rhs=mk[:], start=False, stop=True,
                     reads=[idbR, mkR], writes=[stR[b]], sig=True)
            S.op("act", lambda e: e.activation(out=pt[b][:, 0:N], in_=st[b][:, 0:N], func=AF.Exp,
                                               scale=1.0, bias=negF[:, j, h:h + 1]),
                 reads=[stR[b], negFR], writes=[ptR[b]])

        def emit_pv(n):
            h, c, j = its[n]
            b = n % 4
            par = (h * NQC + c) % 2
            qb0 = max(j, 4 * c)
            for i in range(qb0, 4 * c + 4):
                li = i - 4 * c
                off = (i - qb0) * 128
                last = (i == 4 * c + 3)
                S.op("pe", lambda e, par=par, li=li, off=off, i=i: e.matmul(
                    ac[par][:, li * 65:(li + 1) * 65], lhsT=pt[b][:, off:off + 128], rhs=va[:, j, h * 65:(h + 1) * 65],
                    start=(j == 0 and li == 0), stop=(j == i), skip_group_check=True),
                     reads=[ptR[b], vaR], writes=[acR[par]], sig=last)
            if j == 4 * c + 3:
                for li in range(4):
                    i = 4 * c + li
                    S.op("dve", lambda e, par=par, li=li: e.reciprocal(out=rc[:, li:li + 1],
                                                                        in_=ac[par][:, li * 65 + 64:li * 65 + 65]),
                         reads=[acR[par]], writes=[rcR])
                    S.op("dve", lambda e, par=par, li=li, i=i, h=h: e.scalar_tensor_tensor(
                        out=ybh[h % 2][:, i, :], in0=ac[par][:, li * 65:li * 65 + 64], scalar=rc[:, li:li + 1],
                        in1=sg[:, i, h * 64:(h + 1) * 64], op0=ALU.mult, op1=ALU.mult),
                         reads=[acR[par], rcR, sgR], writes=[ybhR[h % 2]])
            if c == NQC - 1 and j == 4 * c + 3:
                S.dma("sp", yb[:, h * 64:(h + 1) * 64].rearrange("(t p) c -> p t c", p=128), ybh[h % 2][:],
                      reads=[ybhR[h % 2]], is_output=True)

        LAG = 2
        load_head(0)
        nxt_head = 1
        for n in range(len(its) + LAG):
            if n < len(its):
                h, c, j = its[n]
                if c == 0 and j == 0 and nxt_head < NH2 and h + 1 == nxt_head:
                    load_head(nxt_head)
                    nxt_head += 1
                emit_scores(n)
            if n - LAG >= 0:
                emit_pv(n - LAG)
        S.finish()
    return nc


def build_l3(final):
    nc = bass.Bass("TRN2", target_bir_lowering=False)
    x = _dram(nc, "x", [TOK, D], F32, "ExternalInput")
    ya = _dram(nc, "ya", [TOK, WA], BF16, "ExternalInput")
    yb = _dram(nc, "yb", [TOK, WB], BF16, "ExternalInput")
    wout = _dram(nc, "wout", [D, D], F32, "ExternalInput")
    gateB = _dram(nc, "gateB", [128, D], F32, "ExternalInput")
    fg = _dram(nc, "fg", [128, D], F32, "ExternalInput")
    identb = _dram(nc, "identb", [128, 128], BF16, "ExternalInput")
    xo = _dram(nc, "xo", [TOK, D], F32, "ExternalOutput")

    es = contextlib.ExitStack()
    with es:
        S = Sched(nc, es)
        sb = lambda name, shape, dtype: es.enter_context(nc.sbuf_tensor(name, list(shape), dtype))
        pst = lambda name, shape, dtype: es.enter_context(nc.psum_tensor(name, list(shape), dtype))
        wo = sb("wo", [128, 16, D], BF16)
        gB = sb("gB", [128, D], F32)
        fgt = sb("fgt", [128, D], F32)
        ident = sb("ident", [128, 128], BF16)
        xt = [sb("xt%d" % i, [128, D], F32) for i in range(2)]
        yt = [sb("yt%d" % i, [128, D], BF16) for i in range(2)]
        yTt = [sb("yT%d" % i, [128, 16, 128], BF16) for i in range(2)]
        tmp = [sb("tmp%d" % i, [128, 512], F32) for i in range(2)]
        small = sb("small", [128, 4], F32)
        junk = sb("junk", [128, D], BF16)
        acc = [pst("acc%d" % i, [128, 512], F32) for i in range(4)]
        tp = [pst("tp%d" % i, [128, 1024], BF16) for i in range(2)]
        R = Reg
        woR = [R("wo%d" % i) for i in range(4)]
        gBR, fgR, idR = R("gB"), R("fg"), R("id")
        xtR = [R("xt0"), R("xt1")]
        ytR = [R("yt0"), R("yt1")]
        yTR = [R("yT0"), R("yT1")]
        tmpR = [R("tmp0"), R("tmp1")]
        smallR, junkR = R("small"), R("junk")
        accR = [R("acc%d" % i) for i in range(4)]
        tpR = [R("tp0"), R("tp1")]

        S.dma("sp", gB[:], gateB[:, :], writes=[gBR])
        S.dma("sp", ident[:], identb[:, :], writes=[idR])
        if final:
            S.dma("sp", fgt[:], fg[:, :], writes=[fgR])
        for cc in range(4):
            S.dma("pool", wo[:, :, cc * 512:(cc + 1) * 512],
                  wout[:, cc * 512:(cc + 1) * 512].rearrange("(k p) c -> p k c", p=128), writes=[woR[cc]])
        ai = 0
        ti = 0
        mi = 0
        for t in range(TOK // 128):
            s = t % 2
            r0 = t * 128
            S.dma("sp", xt[s][:], x[r0:r0 + 128, :], writes=[xtR[s]])
            S.dma("sp", yt[s][:, 0:WA], ya[r0:r0 + 128, :], writes=[ytR[s]])
            S.dma("sp", yt[s][:, WA:D], yb[r0:r0 + 128, :], writes=[ytR[s]])
            for g in range(4):
                tpi = ti % 2
                ti += 1
                for kk in range(4):
                    k = g * 4 + kk
                    S.op("pe", lambda e, tpi=tpi, kk=kk, k=k, s=s: e.transpose(
                        tp[tpi][:, kk * 128:(kk + 1) * 128], yt[s][:, k * 128:(k + 1) * 128], ident[:]),
                         reads=[ytR[s], idR], writes=[tpR[tpi]], sig=(kk == 3))
                src = tp[tpi][:, 0:512].rearrange("p (k t) -> p k t", t=128)
                dst = yTt[s][:, g * 4:(g + 1) * 4, :]
                S.op("act", lambda e, src=src, dst=dst: e.copy(out=dst, in_=src), reads=[tpR[tpi]], writes=[yTR[s]])
            for cc in range(4):
                a = ai % 4
                ai += 1
                for k in range(16):
                    S.mm(acc[a][:, :], lhsT=yTt[s][:, k, :], rhs=wo[:, k, cc * 512:(cc + 1) * 512], start=(k == 0),
                         stop=(k == 15), reads=[yTR[s], woR[cc]], writes=[accR[a]], sig=(k == 15))
                m = mi % 2
                mi += 1
                S.op("dve", lambda e, a=a, m=m, cc=cc: e.tensor_tensor(out=tmp[m][:], in0=acc[a][:, :],
                                                                       in1=gB[:, cc * 512:(cc + 1) * 512], op=ALU.mult),
                     reads=[accR[a], gBR], writes=[tmpR[m]])
                S.op("pool", lambda e, m=m, s=s, cc=cc: e.tensor_tensor(
                    out=xt[s][:, cc * 512:(cc + 1) * 512], in0=xt[s][:, cc * 512:(cc + 1) * 512], in1=tmp[m][:], op=ALU.add),
                     reads=[tmpR[m], xtR[s]], writes=[xtR[s]])
            if final:
                S.op("act", lambda e, s=s: e.activation(out=junk[:], in_=xt[s][:], func=AF.Square, accum_out=small[:, 0:1]),
                     reads=[xtR[s]], writes=[junkR, smallR])
                S.op("act", lambda e: e.activation(out=small[:, 1:2], in_=small[:, 0:1], func=AF.Sqrt, scale=1.0 / D,
                                                   bias=EPS),
                     reads=[smallR], writes=[smallR])
                S.op("dve", lambda e: e.reciprocal(out=small[:, 2:3], in_=small[:, 1:2]), reads=[smallR], writes=[smallR])
                S.op("dve", lambda e, s=s: e.scalar_tensor_tensor(out=xt[s][:], in0=xt[s][:], scalar=small[:, 2:3],
                                                                  in1=fgt[:], op0=ALU.mult, op1=ALU.mult),
                     reads=[xtR[s], smallR, fgR], writes=[xtR[s]])
            S.dma("sp", xo[r0:r0 + 128, :], xt[s][:], reads=[xtR[s]], is_output=True)
        S.finish()
    return nc


_CACHE = {}


def _get(name, fn):
    if name not in _CACHE:
        _CACHE[name] = fn()
    return _CACHE[name]


def _consts():
    tri = np.triu(np.ones((128, 128), np.float32))
    identf = np.eye(128, dtype=np.float32)
    identb = np.eye(128).astype(NPBF)
    p = np.arange(128)[:, None]
    f = np.arange(128)[None, :]
    maskd = np.where(p > f, MASKV, 0.0).astype(NPBF)
    return tri, identf, identb, maskd


def run_l1(x_cur, c, norm_g, w_ada, b_ada, w_in, ln_v_g, ln_v_b, w_s, b_s, b_f, l):
    tri, identf, identb, maskd = _consts()
    nc = _get("l1", build_l1)
    rep = lambda v: np.ascontiguousarray(np.broadcast_to(v[None, :], (128, v.shape[0])))
    common = {
        "ng": rep(norm_g[l]), "wada": np.ascontiguousarray(w_ada[l]), "bada": np.ascontiguousarray(b_ada[l][None, :]),
        "win": np.ascontiguousarray(w_in[l]), "lng": rep(ln_v_g[l]), "lnb": rep(ln_v_b[l]),
        "wsT": np.ascontiguousarray(np.transpose(w_s[l], (2, 0, 1))),
        "tri": tri, "bs": np.ascontiguousarray(b_s[l].T), "bfc": np.ascontiguousarray(b_f[l][:, None]),
        "identb": identb,
    }
    in_maps = []
    for core in range(NCORES):
        b, half = core // 2, core % 2
        m = dict(common)
        m["x"] = np.ascontiguousarray(x_cur[b, half * TOK:(half + 1) * TOK, :])
        m["cT"] = np.ascontiguousarray(c[b].reshape(16, 128).T)
        in_maps.append(m)
    res = run_bass_kernel_spmd(nc, in_maps, core_ids=list(range(NCORES)))
    return res.results


def run_l2(r1):
    tri, identf, identb, maskd = _consts()
    nc = _get("l2", build_l2)
    in_maps = []
    for core in range(NCORES):
        b, hh = core // 2, core % 2
        r_lo, r_hi = r1[2 * b], r1[2 * b + 1]
        cat = lambda key, ax: np.concatenate([r_lo[key], r_hi[key]], axis=ax)
        m = {
            "qT": np.ascontiguousarray(cat("qT", 1)[hh * 512:(hh + 1) * 512, :]),
            "kT": np.ascontiguousarray(cat("kT", 1)[hh * 512:(hh + 1) * 512, :]),
            "vv": np.ascontiguousarray(cat("vv", 0)[:, hh * 520:(hh + 1) * 520]),
            "sgb": np.ascontiguousarray(cat("sgb", 0)[:, hh * 512:(hh + 1) * 512]),
            "lfT": np.ascontiguousarray(cat("lfT", 1)[hh * 8:(hh + 1) * 8, :]),
            "identf": identf, "identb": identb, "maskd": maskd,
        }
        in_maps.append(m)
    res = run_bass_kernel_spmd(nc, in_maps, core_ids=list(range(NCORES)))
    return res.results


def run_l3(x_cur, r1, r2, w_out, final_g, l, final):
    tri, identf, identb, maskd = _consts()
    nc = _get("l3f" if final else "l3", lambda: build_l3(final))
    wo = np.ascontiguousarray(w_out[l])
    fg = np.ascontiguousarray(np.broadcast_to(final_g[None, :], (128, D)))
    in_maps = []
    for core in range(NCORES):
        b, half = core // 2, core % 2
        ybfull = np.concatenate([r2[2 * b]["yb"], r2[2 * b + 1]["yb"]], axis=1)
        m = {
            "x": np.ascontiguousarray(x_cur[b, half * TOK:(half + 1) * TOK, :]),
            "ya": r1[core]["ya"],
            "yb": np.ascontiguousarray(ybfull[half * TOK:(half + 1) * TOK, :]),
            "wout": wo,
            "gateB": np.ascontiguousarray(np.broadcast_to(r1[core]["gate"], (128, D))),
            "fg": fg, "identb": identb,
        }
        in_maps.append(m)
    res = run_bass_kernel_spmd(nc, in_maps, core_ids=list(range(NCORES)))
    out = np.empty((BATCH, SEQ, D), np.float32)
    for core in range(NCORES):
        b, half = core // 2, core % 2
        out[b, half * TOK:(half + 1) * TOK, :] = res.results[core]["xo"]
    return out


def kernel(x, c, norm_g, w_ada, b_ada, w_in, ln_v_g, ln_v_b, w_s, b_s, b_f, w_out, final_g):
    f = lambda a: np.asarray(a, dtype=np.float32)
    x, c, norm_g, w_ada, b_ada, w_in = f(x), f(c), f(norm_g), f(w_ada), f(b_ada), f(w_in)
    ln_v_g, ln_v_b, w_s, b_s, b_f, w_out, final_g = f(ln_v_g), f(ln_v_b), f(w_s), f(b_s), f(b_f), f(w_out), f(final_g)
    x_cur = x
    for l in range(DEPTH):
        r1 = run_l1(x_cur, c, norm_g, w_ada, b_ada, w_in, ln_v_g, ln_v_b, w_s, b_s, b_f, l)
        r2 = run_l2(r1)
        x_cur = run_l3(x_cur, r1, r2, w_out, final_g, l, final=(l == DEPTH - 1))
    return x_cur
```
